# Optimizing a Trainium2 kernel written in Bass

```python
import jax, jax.numpy as jnp
from jax import lax
import numpy as np

D_MODEL = 2048
BATCH = 4
SEQ = 4096
DEPTH = 4

PLE_DIM = 256
POOL_EXPAND = 2
E_POOL = POOL_EXPAND * D_MODEL
POOL_WINDOWS = (2, 4, 8, 16)
N_POOL_GROUPS = len(POOL_WINDOWS)
G_POOL = E_POOL // N_POOL_GROUPS
N_HEADS = D_MODEL // 128
Q_RANK = 512
KV_RANK = 512
NOPE_DIM = 128
ROPE_DIM = 64
V_DIM = 128
QK_DIM = NOPE_DIM + ROPE_DIM
E_MLA = N_HEADS * V_DIM
ROPE_THETA = 10000.0
Q_BLOCK = 128
SM_SCALE = QK_DIM ** -0.5
NEG_INF = -1e30
EPS = 1e-6
N_POOL_LAYERS = (DEPTH + 1) // 2
N_MLA_LAYERS = DEPTH // 2
MLA_IN_DIM = Q_RANK + KV_RANK + ROPE_DIM + E_MLA

kernel_name = "hybrid_pool_mla_sandwich_ple"


def rms_norm(x, g):
    xf = x.astype(jnp.float32)
    y = xf * lax.rsqrt(jnp.mean(xf * xf, axis=-1, keepdims=True) + EPS)
    return (y * g.astype(jnp.float32)).astype(x.dtype)


def rope(x, cos, sin):
    x1, x2 = jnp.split(x.astype(jnp.float32), 2, axis=-1)
    out = jnp.concatenate([x1 * cos - x2 * sin, x2 * cos + x1 * sin], axis=-1)
    return out.astype(x.dtype)


def pool_mixer(xn, w_in, w_group, scale, w_out):
    B, S, _ = xn.shape
    z = xn @ w_in
    u, g = jnp.split(z, 2, axis=-1)
    uf = u.astype(jnp.float32)
    csum = jnp.concatenate([jnp.zeros((B, 1, E_POOL), jnp.float32), jnp.cumsum(uf, axis=1)], axis=1)
    t = jnp.arange(S)
    pooled = []
    for j, w in enumerate(POOL_WINDOWS):
        sl = slice(j * G_POOL, (j + 1) * G_POOL)
        c = csum[:, :, sl]
        lagged = jnp.concatenate([jnp.zeros((B, w - 1, G_POOL), jnp.float32), c[:, :S - w + 1]], axis=1)
        count = jnp.minimum(t + 1, w).astype(jnp.float32)[None, :, None]
        pooled.append((c[:, 1:] - lagged) / count - uf[:, :, sl])
    pooled = jnp.stack(pooled, axis=2).astype(xn.dtype)
    mixed = jnp.einsum('bsgc,gcd->bsgd', pooled, w_group).reshape(B, S, E_POOL)
    return (mixed * scale * jax.nn.silu(g)) @ w_out


def causal_block_attention(q, k, v):
    B, S, H, Dq = q.shape
    nb = S // Q_BLOCK
    qb = q.reshape(B, nb, Q_BLOCK, H, Dq).transpose(1, 0, 2, 3, 4)
    kpos = jnp.arange(S)

    def one_block(args):
        qi, blk = args
        qpos = blk * Q_BLOCK + jnp.arange(Q_BLOCK)
        s = jnp.einsum('bqhd,bkhd->bhqk', qi, k, preferred_element_type=jnp.float32) * SM_SCALE
        s = jnp.where(kpos[None, :] <= qpos[:, None], s, NEG_INF)
        pr = jax.nn.softmax(s, axis=-1).astype(v.dtype)
        return jnp.einsum('bhqk,bkhd->bqhd', pr, v)

    out = lax.map(one_block, (qb, jnp.arange(nb)))
    return out.transpose(1, 0, 2, 3, 4).reshape(B, S, H, V_DIM)


def mla_mixer(xn, cos, sin, w_in, q_norm, w_uq, kv_norm, w_ukv, w_out):
    B, S, _ = xn.shape
    z = xn @ w_in
    q_lat = z[..., :Q_RANK]
    kv_lat = z[..., Q_RANK:Q_RANK + KV_RANK]
    k_pe = z[..., Q_RANK + KV_RANK:Q_RANK + KV_RANK + ROPE_DIM]
    g = z[..., Q_RANK + KV_RANK + ROPE_DIM:]
    q = (rms_norm(q_lat, q_norm) @ w_uq).reshape(B, S, N_HEADS, QK_DIM)
    kv = (rms_norm(kv_lat, kv_norm) @ w_ukv).reshape(B, S, N_HEADS, NOPE_DIM + V_DIM)
    q_nope, q_pe = q[..., :NOPE_DIM], q[..., NOPE_DIM:]
    k_nope, v = kv[..., :NOPE_DIM], kv[..., NOPE_DIM:]
    q_pe = rope(q_pe, cos[:, :, None, :], sin[:, :, None, :])
    k_pe = rope(k_pe, cos, sin)
    q = jnp.concatenate([q_nope, q_pe], axis=-1)
    k = jnp.concatenate([k_nope, jnp.broadcast_to(k_pe[:, :, None, :], (B, S, N_HEADS, ROPE_DIM))], axis=-1)
    o = causal_block_attention(q, k, v).reshape(B, S, E_MLA)
    return (o * jax.nn.silu(g)) @ w_out


def setup_inputs(seed: int = 0) -> dict:
    key = jax.random.key(seed)
    ks = jax.random.split(key, 24)
    f32 = jnp.float32

    def nrm(k, shape, fan_in):
        return jax.random.normal(k, shape, f32) * (fan_in ** -0.5)

    def gain(k, shape):
        return 1.0 + 0.02 * jax.random.normal(k, shape, f32)

    offs = jax.random.randint(ks[2], (BATCH, 1), 0, 1024, dtype=jnp.int32)
    positions = offs + jnp.arange(SEQ, dtype=jnp.int32)[None, :]
    return {
        "x": jax.random.normal(ks[0], (BATCH, SEQ, D_MODEL), f32),
        "p": jax.random.normal(ks[1], (DEPTH, BATCH, SEQ, PLE_DIM), f32),
        "positions": positions,
        "pre_norm": gain(ks[3], (DEPTH, D_MODEL)),
        "post_norm": gain(ks[4], (DEPTH, D_MODEL)),
        "pool_w_in": nrm(ks[5], (N_POOL_LAYERS, D_MODEL, 2 * E_POOL), D_MODEL),
        "pool_w_group": nrm(ks[6], (N_POOL_LAYERS, N_POOL_GROUPS, G_POOL, G_POOL), G_POOL),
        "pool_scale": gain(ks[7], (N_POOL_LAYERS, E_POOL)),
        "pool_w_out": nrm(ks[8], (N_POOL_LAYERS, E_POOL, D_MODEL), E_POOL),
        "mla_w_in": nrm(ks[9], (N_MLA_LAYERS, D_MODEL, MLA_IN_DIM), D_MODEL),
        "mla_q_norm": gain(ks[10], (N_MLA_LAYERS, Q_RANK)),
        "mla_w_uq": nrm(ks[11], (N_MLA_LAYERS, Q_RANK, N_HEADS * QK_DIM), Q_RANK),
        "mla_kv_norm": gain(ks[12], (N_MLA_LAYERS, KV_RANK)),
        "mla_w_ukv": nrm(ks[13], (N_MLA_LAYERS, KV_RANK, N_HEADS * (NOPE_DIM + V_DIM)), KV_RANK),
        "mla_w_out": nrm(ks[14], (N_MLA_LAYERS, E_MLA, D_MODEL), E_MLA),
        "ple_norm": gain(ks[15], (DEPTH, D_MODEL)),
        "ple_w_gate": nrm(ks[16], (DEPTH, D_MODEL, D_MODEL), D_MODEL),
        "ple_w_proj": nrm(ks[17], (DEPTH, PLE_DIM, D_MODEL), PLE_DIM),
    }


def reference(x, p, positions, pre_norm, post_norm, pool_w_in, pool_w_group, pool_scale, pool_w_out,
              mla_w_in, mla_q_norm, mla_w_uq, mla_kv_norm, mla_w_ukv, mla_w_out,
              ple_norm, ple_w_gate, ple_w_proj):
    inv_freq = ROPE_THETA ** (-jnp.arange(0, ROPE_DIM, 2, dtype=jnp.float32) / ROPE_DIM)
    ang = positions.astype(jnp.float32)[..., None] * inv_freq
    cos, sin = jnp.cos(ang), jnp.sin(ang)
    h = x
    for i in range(DEPTH):
        xn = rms_norm(h, pre_norm[i])
        j = i // 2
        if i % 2 == 0:
            out = pool_mixer(xn, pool_w_in[j], pool_w_group[j], pool_scale[j], pool_w_out[j])
        else:
            out = mla_mixer(xn, cos, sin, mla_w_in[j], mla_q_norm[j], mla_w_uq[j],
                            mla_kv_norm[j], mla_w_ukv[j], mla_w_out[j])
        h = h + rms_norm(out, post_norm[i])
        gate = jax.nn.sigmoid(rms_norm(h, ple_norm[i]) @ ple_w_gate[i])
        h = h + (p[i] @ ple_w_proj[i]) * gate
    return h
```

```python
import math
from contextlib import ExitStack

import numpy as np
import concourse.bass as bass
import concourse.mybir as mybir
from concourse.bass_utils import run_bass_kernel_spmd

F32 = mybir.dt.float32
BF16 = mybir.dt.bfloat16
I32 = mybir.dt.int32
AF = mybir.ActivationFunctionType
ALU = mybir.AluOpType

NCORE = 8
D = 2048
SEQ = 4096
NTOK = 2048
T = 512
NTI = NTOK // T
KC = D // 128
EPS = 1e-6
NS = 3
WSLOT = 8192
SM_SCALE = 192 ** -0.5
HB = 320
R1 = 16 * HB + 64 + NTOK
KPE0 = 16 * HB
V0 = KPE0 + 64

V_PRE, V_POST, V_PLE, V_SCALE, V_QN, V_KVN, V_IFREQ, V_SSIGN, V_HALF, V_IC = 0, 64, 128, 192, 256, 264, 272, 273, 274, 276
NV = 352

WSHAPES = {
    "pool_w_in": (2 * 2048, 8192),
    "pool_w_group": (2 * 4 * 1024, 1024),
    "pool_w_out": (2 * 4096, 2048),
    "mla_w_in": (2 * 2048, 3200),
    "mla_w_uq": (2 * 512, 4096),
    "mla_w_ukv": (2 * 512, 4096),
    "mla_w_out": (2 * 2048, 2048),
    "ple_w_gate": (4 * 2048, 2048),
    "ple_w_proj": (4 * 256, 2048),
}


class Sem:
    __slots__ = ("h", "cnt")


class Buf:
    __slots__ = ("w", "r")

    def __init__(self):
        self.w = None
        self.r = {}


def bufs(n):
    return [Buf() for _ in range(n)]


class KB:
    ENG = ("pe", "act", "dve", "pool", "sp")

    def __init__(self, plan):
        self.nc = nc = bass.Bass("TRN2", target_bir_lowering=False)
        self.e = dict(pe=nc.tensor, act=nc.scalar, dve=nc.vector, pool=nc.gpsimd, sp=nc.sync)
        self.nsem = 0
        self.esem = {}
        self.waited = {k: {} for k in self.ENG}
        self.dsems = []
        self.dsem_by_name = {}
        self.ccs = []
        self.epoch = 0
        self.new_epoch()
        self.plan = plan
        self.wrec = []
        self.wi = 0
        self.wissued = 0
        self.psi = 0
        self.pslist = list(range(7))

    def sem(self, name):
        S = Sem()
        S.h = self.nc.alloc_semaphore(f"{name}_{self.nsem}")
        self.nsem += 1
        S.cnt = 0
        return S

    def dsem(self, name):
        if name in self.dsem_by_name:
            return self.dsem_by_name[name]
        S = self.sem(name)
        self.dsems.append(S)
        self.dsem_by_name[name] = S
        return S

    def new_epoch(self):
        for k in self.ENG:
            self.esem[k] = self.sem(f"e{k}")
        self.epoch += 1

    def _deps(self, reads, writes):
        d = {}
        raw = {}
        for b in reads:
            if b.w is not None:
                S, v = b.w
                if d.get(S, 0) < v:
                    d[S] = v
                if raw.get(S, 0) < v:
                    raw[S] = v
        for b in writes:
            if b.w is not None:
                S, v = b.w
                if d.get(S, 0) < v:
                    d[S] = v
            for S, v in b.r.items():
                if d.get(S, 0) < v:
                    d[S] = v
        return d, raw

    def _pending(self, k, dr):
        d, raw = dr
        own = self.esem[k]
        W = self.waited[k]
        out = []
        for S, v in d.items():
            if S is own:
                if k == "pe":
                    continue
                v = raw.get(S, 0)
                if not v:
                    continue
            if W.get(S, 0) >= v:
                continue
            out.append((S, v))
            W[S] = v
        return out

    def _wait(self, k, dr, inline=False):
        p = self._pending(k, dr)
        last = None
        if inline and p:
            last = p.pop()
        for S, v in p:
            self.e[k].wait_ge(S.h, v)
        return last

    @staticmethod
    def _mark(ev, reads, writes):
        S, v = ev
        for b in reads:
            if b.r.get(S, 0) < v:
                b.r[S] = v
        for b in writes:
            b.w = ev
            b.r = {}

    def op(self, k, fn, reads=(), writes=()):
        w = self._wait(k, self._deps(reads, writes), inline=True)
        ins = fn(self.e[k])
        if w is not None:
            ins._wait_ge(w[0].h, w[1])
        S = self.esem[k]
        S.cnt += 1
        ins.then_inc(S.h, 1)
        self._mark((S, S.cnt), reads, writes)

    def mm(self, out_ap, pairs, reads, writes, start=True, stop=True):
        w = self._wait("pe", self._deps(reads, writes), inline=True)
        n = len(pairs)
        ins = None
        for i, (l, r) in enumerate(pairs):
            ins = self.nc.tensor.matmul(out_ap, l, r, start=(start and i == 0), stop=(stop and i == n - 1))
            if i == 0 and w is not None:
                ins._wait_ge(w[0].h, w[1])
        S = self.esem["pe"]
        S.cnt += 1
        ins.then_inc(S.h, 1)
        self._mark((S, S.cnt), reads, writes)

    def dma(self, k, out, in_, ds, reads=(), writes=()):
        w = self._wait(k, self._deps(reads, writes), inline=True)
        ins = self.e[k].dma_start(out=out, in_=in_)
        if w is not None:
            ins._wait_ge(w[0].h, w[1])
        ds.cnt += 16
        ins.then_inc(ds.h, 16)
        self._mark((ds, ds.cnt), reads, writes)

    def barrier(self, rotate=True):
        d = {}
        for S in list(self.esem.values()) + self.dsems + self.ccs:
            if S.cnt:
                d[S] = S.cnt
        for k in self.ENG:
            self._wait(k, (d, {}))
            self.waited[k][self.esem[k]] = self.esem[k].cnt
        if rotate:
            self.new_epoch()

    def allgather(self, in_ap, out_ap):
        S = self.sem("cc")
        ins = self.nc.gpsimd.collective_compute(
            "AllGather", ALU.bypass, replica_groups=[list(range(NCORE))],
            ins=[in_ap.opt()], outs=[out_ap.opt()])
        ins.then_inc(S.h)
        S.cnt = 1
        self.ccs.append(S)
        self._wait("pool", ({S: 1}, {}))

    def psum(self):
        i = self.pslist[self.psi % len(self.pslist)]
        self.psi += 1
        return self.ps[i], self.psb[i]

    def wv(self, slot, k, n):
        return self.wsl[slot][:, 0:k * n].rearrange("p (k n) -> p k n", k=k)

    def wsrc(self, name, r0, rows, c0, cols):
        return self.W[name][r0:r0 + rows, c0:c0 + cols].rearrange("(k p) n -> p k n", p=128)

    def wfetch(self, fn):
        i = self.wi
        self.wi += 1
        if self.plan is None:
            self.wrec.append(fn)
            self._wissue(i, fn)
        else:
            lim = min(i + NS, len(self.plan))
            while self.wissued < lim:
                self._wissue(self.wissued, self.plan[self.wissued])
                self.wissued += 1
        return i % NS

    def _wissue(self, j, fn):
        slot = j % NS
        for dst, src in fn(self, slot):
            self.dma("sp", dst, src, self.wds[slot], writes=[self.wbuf[slot]])

    def wblock(self, name, r0, rows, c0, cols):
        k = rows // 128
        assert k * cols <= WSLOT
        slot = self.wfetch(lambda kb, s: [(kb.wv(s, k, cols), kb.wsrc(name, r0, rows, c0, cols))])
        return slot, self.wv(slot, k, cols)


def build(nlayers=4, plan=None, l0=0):
    kb = KB(plan)
    nc = kb.nc
    dt_in = lambda n, s, d: nc.dram_tensor(n, s, d, kind="ExternalInput").ap()
    x_fm = dt_in("x_fm", [D, NTOK], F32)
    xhalo = dt_in("xhalo", [D, 16], F32)
    p_fm = dt_in("p_fm", [4 * 256, NTOK], F32)
    pos_in = dt_in("pos", [64, NTOK], I32)
    vecs_in = dt_in("vecs", [128, NV], F32)
    wsh = {n: dt_in("w_" + n, [r // NCORE, c], F32) for n, (r, c) in WSHAPES.items()}
    out_fm = nc.dram_tensor("out_fm", [D, NTOK], F32, kind="ExternalOutput").ap()

    wcv = {n: nc.dram_tensor("cv_" + n, [r // NCORE, c], BF16).ap() for n, (r, c) in WSHAPES.items()}
    kb.W = {n: nc.dram_tensor("wf_" + n, [r, c], BF16).ap() for n, (r, c) in WSHAPES.items()}
    hbuf = nc.dram_tensor("hbuf", [D, NTOK], F32).ap()
    gbuf = nc.dram_tensor("gbuf", [D, NTOK], BF16).ap()
    send1 = nc.dram_tensor("send1", [R1, NTOK], BF16).ap()
    gath1 = nc.dram_tensor("gath1", [NCORE * R1, NTOK], BF16).ap()
    send2 = nc.dram_tensor("send2", [NCORE * 256, NTOK], BF16).ap()
    gath2 = nc.dram_tensor("gath2", [NCORE * NCORE * 256, NTOK], BF16).ap()
    myq = nc.dram_tensor("myq", [NCORE, 2 * HB, NTOK], BF16).ap()
    myv = nc.dram_tensor("myv", [NCORE, NTOK, 256], BF16).ap()
    myo = nc.dram_tensor("myo", [NCORE, 256, NTOK], BF16).ap()
    myhalo = nc.dram_tensor("myhalo", [D, 16], F32).ap()
    tail = nc.dram_tensor("tail", [D, 16], F32).ap()
    gath_t = nc.dram_tensor("gath_t", [NCORE * D, 16], F32).ap()

    uid = [0]

    def A(n, sh, d):
        uid[0] += 1
        return nc.alloc_sbuf_tensor(f"sb{uid[0]}_{n}", sh, d)
    kb.wsl = [A(f"wsl{i}", [128, WSLOT], BF16) for i in range(NS)]
    kb.wbuf = bufs(NS)
    kb.wds = [kb.dsem(f"wds{i}") for i in range(NS)]
    vecs = A("vecs", [128, NV], F32)
    vecs_b = Buf()
    ones = A("ones", [128, 128], F32)
    onesb = A("onesb", [128, 128], BF16)
    ones_b = Buf()
    sq = [A(f"sq{i}", [128, T], BF16) for i in range(2)]
    sq_b = bufs(2)
    rstd = [A(f"rstd{i}", [128, T], F32) for i in range(2)]
    rstd_b = bufs(2)
    tmp = [A(f"tmp{i}", [128, T], F32) for i in range(2)]
    tmp_b = bufs(2)
    kb.ps = [nc.alloc_psum_tensor(f"ps{i}", [128, 512], F32) for i in range(8)]
    kb.psb = bufs(8)
    misc_ds = kb.dsem("misc")

    tri = A("tri", [128, 128], BF16)
    trif = A("trif", [128, 128], F32)
    trib = Buf()
    kb.op("dve", lambda e: e.memset(trif[:], 1.0), writes=[trib])
    kb.op("pool", lambda e: e.affine_select(out=trif[:], in_=trif[:], pattern=[[1, 128]],
                                            compare_op=ALU.is_ge, fill=0.0, base=0, channel_multiplier=-1),
          reads=[trib], writes=[trib])
    kb.op("dve", lambda e: e.tensor_copy(out=tri[:], in_=trif[:]), reads=[trib], writes=[trib])
    kb.dma("sp", vecs[:], vecs_in, misc_ds, writes=[vecs_b])
    kb.op("dve", lambda e: e.memset(ones[:], 1.0), writes=[ones_b])
    kb.op("dve", lambda e: e.memset(onesb[:], 1.0), writes=[ones_b])

    cv_ds = kb.dsem("cv")
    for n, (r, c) in WSHAPES.items():
        rs = r // NCORE
        for r0 in range(0, rs, 128):
            r1 = min(rs, r0 + 128)
            kb.dma("pool", wcv[n][r0:r1, :], wsh[n][r0:r1, :], cv_ds)
    kb.barrier(rotate=False)
    for n in WSHAPES:
        kb.allgather(wcv[n], kb.W[n])
    kb.barrier()

    vcol = lambda c: vecs[:, c:c + 1]

    def norm_rstd(src_aps, src_bufs, dim, ri, ncols=T):
        ps, pb = kb.ps[7], kb.psb[7]
        n = len(src_aps)
        for i in range(n):
            s_, sb_ = sq[i % 2], sq_b[i % 2]
            a = src_aps[i]
            kb.op("act", lambda e: e.activation(out=s_[:, :ncols], in_=a, func=AF.Square),
                  reads=[src_bufs[i]], writes=[sb_])
            kb.mm(ps[:, :ncols], [(onesb[:], s_[:, :ncols])], reads=[sb_, ones_b], writes=[pb],
                  start=(i == 0), stop=(i == n - 1))
        r_, rb_ = rstd[ri], rstd_b[ri]
        kb.op("act", lambda e: e.activation(out=r_[:, :ncols], in_=ps[:, :ncols], func=AF.Sqrt,
                                            bias=EPS, scale=1.0 / dim), reads=[pb], writes=[rb_])
        kb.op("dve", lambda e: e.reciprocal(out=r_[:, :ncols], in_=r_[:, :ncols]), reads=[rb_], writes=[rb_])
        return r_, rb_

    def apply_norm(dst_ap, dst_buf, src_ap, src_buf, gcol, r_, rb_, ncols=T):
        kb.op("dve", lambda e: e.scalar_tensor_tensor(out=dst_ap, in0=src_ap, scalar=vcol(gcol), in1=r_[:, :ncols],
                                                      op0=ALU.mult, op1=ALU.mult),
              reads=[src_buf, rb_, vecs_b], writes=[dst_buf])

    h_ds = [kb.dsem("h0"), kb.dsem("h1")]
    p_ds = kb.dsem("p")

    def load_h(hT, hb, src, t0):
        v = src.rearrange("(k p) t -> p k t", p=128)
        for g in range(2):
            kb.dma("sp", hT[:, g * 8:(g + 1) * 8, :], v[:, g * 8:(g + 1) * 8, t0:t0 + T], h_ds[g],
                   writes=hb[g * 8:(g + 1) * 8])

    def store_h(hT, hb, dst, t0):
        v = dst.rearrange("(k p) t -> p k t", p=128)
        for g in range(2):
            kb.dma("sp", v[:, g * 8:(g + 1) * 8, t0:t0 + T], hT[:, g * 8:(g + 1) * 8, :], h_ds[g],
                   reads=hb[g * 8:(g + 1) * 8])

    def post_and_ple(l, hT, hb, ot, otb, xn, pT, pT_b, wproj, wproj_b, gt, gt_b):
        r_, rb_ = norm_rstd([ot[:, m, :] for m in range(KC)], otb, D, 0)
        for m in range(KC):
            t_, tb_ = tmp[m % 2], tmp_b[m % 2]
            apply_norm(t_[:], tb_, ot[:, m, :], otb[m], V_POST + l * 16 + m, r_, rb_)
            kb.op("dve", lambda e: e.tensor_tensor(out=hT[:, m, :], in0=hT[:, m, :], in1=t_[:], op=ALU.add),
                  reads=[tb_, hb[m]], writes=[hb[m]])
        r_, rb_ = norm_rstd([hT[:, m, :] for m in range(KC)], hb, D, 1)
        for kc in range(KC):
            apply_norm(xn(kc), otb[kc // 2], hT[:, kc, :], hb[kc], V_PLE + l * 16 + kc, r_, rb_)
        for blk in range(4):
            slot, wv = kb.wblock("ple_w_gate", l * D, D, blk * 512, 512)
            for cc in range(4):
                m = blk * 4 + cc
                ps, pb = kb.psum()
                kb.mm(ps[:], [(wv[:, kc, cc * 128:(cc + 1) * 128], xn(kc)) for kc in range(KC)],
                      reads=[kb.wbuf[slot]] + otb[0:8], writes=[pb])
                g_, gb_ = gt[m % 2], gt_b[m % 2]
                kb.op("act", lambda e: e.activation(out=g_[:], in_=ps[:], func=AF.Sigmoid), reads=[pb], writes=[gb_])
                ps2, pb2 = kb.psum()
                kb.mm(ps2[:], [(wproj[:, k2, m * 128:(m + 1) * 128], pT[:, k2, :]) for k2 in range(2)],
                      reads=[wproj_b, pT_b], writes=[pb2])
                t_, tb_ = tmp[m % 2], tmp_b[m % 2]
                kb.op("dve", lambda e: e.tensor_tensor(out=t_[:], in0=ps2[:], in1=g_[:], op=ALU.mult),
                      reads=[pb2, gb_], writes=[tb_])
                kb.op("dve", lambda e: e.tensor_tensor(out=hT[:, m, :], in0=hT[:, m, :], in1=t_[:], op=ALU.add),
                      reads=[tb_, hb[m]], writes=[hb[m]])

    def load_layer_consts(l, wproj, wproj_b):
        kb.dma("sp", wproj[:], kb.W["ple_w_proj"][l * 256:(l + 1) * 256, :].rearrange("(k p) n -> p k n", p=128),
               kb.dsem("wproj"), writes=[wproj_b])

    pTf = A("pTf", [128, 2, T], F32)
    pTf_b = Buf()

    def load_p(l, pT, pT_b, t0):
        kb.dma("sp", pTf[:], p_fm[l * 256:(l + 1) * 256, t0:t0 + T].rearrange("(k p) t -> p k t", p=128),
               p_ds, writes=[pTf_b])
        kb.op("act", lambda e: e.copy(out=pT[:], in_=pTf[:]), reads=[pTf_b], writes=[pT_b])

    def pool_layer(l, j, src, dst, halo_from_gath):
        with ExitStack() as st:
            def S_(n, sh, d):
                uid[0] += 1
                return st.enter_context(nc.sbuf_tensor(f"sb{uid[0]}_{n}", sh, d))
            hT = S_("hT", [128, KC, T], F32); hb = bufs(KC)
            ot = S_("ot", [128, KC, T], F32); otb = bufs(KC)
            otv = ot[:].bitcast(BF16)
            xn = lambda kc: otv[:, kc // 2, (kc % 2) * T:(kc % 2 + 1) * T]
            y = S_("y", [128, 32, T], BF16); yb = bufs(32)
            pooled = S_("pooled", [128, 8, T], BF16); plb = bufs(8)
            sg = S_("sg", [128, 8, T], BF16); sgb = bufs(8)
            upad = [S_(f"upad{i}", [128, 16 + T], F32) for i in range(2)]; upb = bufs(2)
            scr = [S_(f"scr{i}", [128, 16 + T], F32) for i in range(2)]; scb = bufs(2)
            halo = S_("halo", [128, 32, 16], F32); halob = bufs(32)
            hh = S_("hh", [128, KC, 16], F32); hhb = bufs(KC)
            xhn = S_("xhn", [128, KC, 16], BF16); xhnb = Buf()
            pT = S_("pT", [128, 2, T], BF16); pT_b = Buf()
            gt = [S_(f"gt{i}", [128, T], F32) for i in range(2)]; gt_b = bufs(2)
            wproj = S_("wproj", [128, 2, D], BF16); wproj_b = Buf()
            t16 = S_("t16", [128, 16], F32); t16b = Buf()
            load_layer_consts(l, wproj, wproj_b)

            if halo_from_gath:
                pid = nc.sync.partition_id()
                myhalo_b = Buf()
                kb.dma("sp", myhalo.rearrange("a b -> (a b)").rearrange("(o n) -> o n", o=1),
                       gath_t.rearrange("(r n) t -> r (n t)", r=NCORE)[bass.ds((pid + 7) % 8, 1)],
                       kb.dsem("loc"), writes=[myhalo_b])
                hsrc = myhalo.rearrange("(k p) t -> p k t", p=128)
                hh_reads = [myhalo_b]
            else:
                hsrc = xhalo.rearrange("(k p) t -> p k t", p=128)
                hh_reads = []
            kb.dma("sp", hh[:], hsrc, kb.dsem("hh"), reads=hh_reads, writes=hhb)
            if halo_from_gath:
                kb.op("dve", lambda e: e.tensor_scalar(out=hh[:], in0=hh[:], scalar1=vcol(V_HALF), scalar2=None,
                                                       op0=ALU.mult), reads=hhb + [vecs_b], writes=hhb)
            r_, rb_ = norm_rstd([hh[:, k, :] for k in range(KC)], hhb, D, 0, ncols=16)
            for k in range(KC):
                apply_norm(xhn[:, k, :], xhnb, hh[:, k, :], hhb[k], V_PRE + l * 16 + k, r_, rb_, ncols=16)
            for blk in range(8):
                slot, wv = kb.wblock("pool_w_in", j * D, D, blk * 512, 512)
                for cc in range(4):
                    c = blk * 4 + cc
                    ps, pb = kb.psum()
                    kb.mm(ps[:, 0:16], [(wv[:, k, cc * 128:(cc + 1) * 128], xhn[:, k, :]) for k in range(KC)],
                          reads=[kb.wbuf[slot], xhnb], writes=[pb])
                    kb.op("act", lambda e: e.copy(out=halo[:, c, :], in_=ps[:, 0:16]), reads=[pb], writes=[halob[c]])

            for ti in range(NTI):
                t0 = ti * T
                load_h(hT, hb, src, t0)
                load_p(l, pT, pT_b, t0)
                r_, rb_ = norm_rstd([hT[:, k, :] for k in range(KC)], hb, D, 0)
                for k in range(KC):
                    apply_norm(xn(k), otb[k // 2], hT[:, k, :], hb[k], V_PRE + l * 16 + k, r_, rb_)
                xnb = otb[0:8]
                for g in range(4):
                    win = 2 ** (g + 1)
                    for blk in range(2):
                        slot, wv = kb.wblock("pool_w_in", j * D, D, g * 1024 + blk * 512, 512)
                        for cc in range(4):
                            cl = blk * 4 + cc
                            c = g * 8 + cl
                            ps, pb = kb.psum()
                            kb.mm(ps[:], [(wv[:, k, cc * 128:(cc + 1) * 128], xn(k)) for k in range(KC)],
                                  reads=[kb.wbuf[slot]] + xnb, writes=[pb])
                            up, ub = upad[c % 2], upb[c % 2]
                            kb.op("act", lambda e: e.copy(out=up[:, 16:16 + T], in_=ps[:]), reads=[pb], writes=[ub])
                            kb.op("dve", lambda e: e.tensor_copy(out=up[:, 0:16], in_=halo[:, c, :]),
                                  reads=[halob[c], ub], writes=[ub])
                            kb.op("dve", lambda e: e.tensor_copy(out=halo[:, c, :], in_=up[:, T:T + 16]),
                                  reads=[ub], writes=[halob[c]])
                            s_ap, s_b = up, ub
                            for stp in range(g + 1):
                                sh = 2 ** stp
                                d_ap, d_b = scr[stp % 2], scb[stp % 2]
                                kb.op("dve", lambda e: e.tensor_tensor(out=d_ap[:, sh:16 + T], in0=s_ap[:, sh:16 + T],
                                                                       in1=s_ap[:, 0:16 + T - sh], op=ALU.add),
                                      reads=[s_b], writes=[d_b])
                                s_ap, s_b = d_ap, d_b
                            kb.op("dve", lambda e: e.scalar_tensor_tensor(out=pooled[:, cl, :], in0=s_ap[:, 16:16 + T],
                                                                          scalar=1.0 / win, in1=up[:, 16:16 + T],
                                                                          op0=ALU.mult, op1=ALU.subtract),
                                  reads=[s_b, ub], writes=[plb[cl]])
                            if ti == 0:
                                kb.op("dve", lambda e: e.tensor_tensor(out=t16[:], in0=s_ap[:, 16:32],
                                                                       in1=vecs[:, V_IC + g * 16:V_IC + g * 16 + 16],
                                                                       op=ALU.mult), reads=[s_b, vecs_b], writes=[t16b])
                                kb.op("dve", lambda e: e.tensor_tensor(out=pooled[:, cl, 0:16], in0=t16[:],
                                                                       in1=up[:, 16:32], op=ALU.subtract),
                                      reads=[t16b, ub], writes=[plb[cl]])
                    for blk in range(2):
                        slot, wv = kb.wblock("pool_w_in", j * D, D, 4096 + g * 1024 + blk * 512, 512)
                        for cc in range(4):
                            cl = blk * 4 + cc
                            ps, pb = kb.psum()
                            kb.mm(ps[:], [(wv[:, k, cc * 128:(cc + 1) * 128], xn(k)) for k in range(KC)],
                                  reads=[kb.wbuf[slot]] + xnb, writes=[pb])
                            kb.op("act", lambda e: e.activation(out=sg[:, cl, :], in_=ps[:], func=AF.Silu),
                                  reads=[pb], writes=[sgb[cl]])
                    slot, wv = kb.wblock("pool_w_group", (j * 4 + g) * 1024, 1024, 0, 1024)
                    for co in range(8):
                        c = g * 8 + co
                        ps, pb = kb.psum()
                        kb.mm(ps[:], [(wv[:, ci, co * 128:(co + 1) * 128], pooled[:, ci, :]) for ci in range(8)],
                              reads=[kb.wbuf[slot]] + plb, writes=[pb])
                        kb.op("dve", lambda e: e.scalar_tensor_tensor(out=y[:, c, :], in0=ps[:],
                                                                      scalar=vcol(V_SCALE + j * 32 + c),
                                                                      in1=sg[:, co, :], op0=ALU.mult, op1=ALU.mult),
                              reads=[pb, sgb[co], vecs_b], writes=[yb[c]])
                for blk in range(8):
                    slot, wv = kb.wblock("pool_w_out", j * 4096, 4096, blk * 256, 256)
                    for cc in range(2):
                        m = blk * 2 + cc
                        ps, pb = kb.psum()
                        kb.mm(ps[:], [(wv[:, k, cc * 128:(cc + 1) * 128], y[:, k, :]) for k in range(32)],
                              reads=[kb.wbuf[slot]] + yb, writes=[pb])
                        kb.op("act", lambda e: e.copy(out=ot[:, m, :], in_=ps[:]), reads=[pb], writes=[otb[m]])
                post_and_ple(l, hT, hb, ot, otb, xn, pT, pT_b, wproj, wproj_b, gt, gt_b)
                store_h(hT, hb, dst, t0)
            kb.barrier()

    def mla_layer(l, j, src, dst, write_tail):
        with ExitStack() as st:
            def S_(n, sh, d):
                uid[0] += 1
                return st.enter_context(nc.sbuf_tensor(f"sb{uid[0]}_{n}", sh, d))
            hT = S_("hT", [128, KC, T], F32); hb = bufs(KC)
            xnt = S_("xnt", [128, KC, T], BF16); xnb = bufs(KC)
            lat = S_("lat", [128, 8, T], F32); latb = bufs(8)
            qn = S_("qn", [128, 4, T], BF16); qnb = bufs(4)
            kvn = S_("kvn", [128, 4, T], BF16); kvnb = bufs(4)
            sgt = S_("sgt", [128, KC, T], BF16); sgtb = bufs(KC); sgt_ds = kb.dsem("sgt")
            stg = [S_(f"stg{i}", [128, T], BF16) for i in range(3)]; stgb = bufs(3)
            stg_ds = [kb.dsem(f"stg{i}") for i in range(3)]
            vst = S_("vst", [128, 4, D], BF16); vstb = bufs(4); vst_ds = kb.dsem("vst")
            posi = S_("posi", [64, T], I32); posib = Buf()
            ang = S_("ang", [64, T], F32); angb = Buf()
            fr = S_("fr", [64, T], F32); frb = Buf()
            fi = S_("fi", [64, T], I32); fib = Buf()
            cos2 = S_("cos2", [64, T], F32); cosb = Buf()
            sin2 = S_("sin2", [64, T], F32); sinb = Buf()
            r1 = S_("r1", [64, T], F32); r1b = Buf()
            r2 = S_("r2", [64, T], F32); r2b = Buf()
            nstg = [0]

            def stage_out(fill, rows, dst_ap):
                i = nstg[0] % 3
                nstg[0] += 1
                fill(stg[i][0:rows, :], stgb[i])
                kb.dma("sp", dst_ap, stg[i][0:rows, :], stg_ds[i], reads=[stgb[i]])

            def frac_sin(out_ap, out_b, offset, neg_rows):
                kb.op("dve", lambda e: e.tensor_scalar(out=fr[:], in0=ang[:], scalar1=1.0 / (2 * math.pi), scalar2=offset,
                                                       op0=ALU.mult, op1=ALU.add), reads=[angb], writes=[frb])
                kb.op("dve", lambda e: e.tensor_copy(out=fi[:], in_=fr[:]), reads=[frb], writes=[fib])
                kb.op("dve", lambda e: e.tensor_copy(out=r1[:], in_=fi[:]), reads=[fib], writes=[r1b])
                kb.op("dve", lambda e: e.tensor_tensor(out=fr[:], in0=fr[:], in1=r1[:], op=ALU.subtract),
                      reads=[r1b, frb], writes=[frb])
                kb.op("dve", lambda e: e.tensor_scalar(out=r1[:], in0=fr[:], scalar1=0.5, scalar2=None, op0=ALU.is_gt),
                      reads=[frb], writes=[r1b])
                kb.op("dve", lambda e: e.tensor_tensor(out=fr[:], in0=fr[:], in1=r1[:], op=ALU.subtract),
                      reads=[r1b, frb], writes=[frb])
                kb.op("dve", lambda e: e.tensor_scalar(out=r1[:], in0=fr[:], scalar1=-0.5, scalar2=None, op0=ALU.is_lt),
                      reads=[frb], writes=[r1b])
                kb.op("dve", lambda e: e.tensor_tensor(out=fr[:], in0=fr[:], in1=r1[:], op=ALU.add),
                      reads=[r1b, frb], writes=[frb])
                kb.op("act", lambda e: e.activation(out=out_ap, in_=fr[:], func=AF.Sin, scale=2 * math.pi),
                      reads=[frb], writes=[out_b])

            def rope(dst_ap, dst_b, psa, pba, psb_, pbb):
                kb.op("dve", lambda e: e.tensor_tensor(out=r1[:], in0=psa[0:64, :], in1=cos2[:], op=ALU.mult),
                      reads=[pba, cosb], writes=[r1b])
                kb.op("dve", lambda e: e.tensor_tensor(out=r2[:], in0=psb_[0:64, :], in1=sin2[:], op=ALU.mult),
                      reads=[pbb, sinb], writes=[r2b])
                kb.op("dve", lambda e: e.tensor_tensor(out=dst_ap, in0=r1[:], in1=r2[:], op=ALU.add),
                      reads=[r1b, r2b], writes=[dst_b])

            for ti in range(NTI):
                t0 = ti * T
                load_h(hT, hb, src, t0)
                kb.dma("sp", posi[:], pos_in[:, t0:t0 + T], kb.dsem("posi"), writes=[posib])
                kb.op("dve", lambda e: e.tensor_copy(out=ang[:], in_=posi[:]), reads=[posib], writes=[angb])
                kb.op("dve", lambda e: e.tensor_scalar(out=ang[:], in0=ang[:], scalar1=vecs[0:64, V_IFREQ:V_IFREQ + 1],
                                                       scalar2=None, op0=ALU.mult), reads=[angb, vecs_b], writes=[angb])
                frac_sin(cos2[:], cosb, 0.25, False)
                frac_sin(sin2[:], sinb, 0.0, True)
                kb.op("dve", lambda e: e.tensor_scalar(out=sin2[:], in0=sin2[:], scalar1=vecs[0:64, V_SSIGN:V_SSIGN + 1],
                                                       scalar2=None, op0=ALU.mult), reads=[sinb, vecs_b], writes=[sinb])
                r_, rb_ = norm_rstd([hT[:, k, :] for k in range(KC)], hb, D, 0)
                for k in range(KC):
                    apply_norm(xnt[:, k, :], xnb[k], hT[:, k, :], hb[k], V_PRE + l * 16 + k, r_, rb_)
                xin = lambda k: xnt[:, k, :]
                for blk in range(2):
                    slot, wv = kb.wblock("mla_w_in", j * D, D, blk * 512, 512)
                    for cc in range(4):
                        ps, pb = kb.psum()
                        kb.mm(ps[:], [(wv[:, k, cc * 128:(cc + 1) * 128], xin(k)) for k in range(KC)],
                              reads=[kb.wbuf[slot]] + xnb, writes=[pb])
                        kb.op("act", lambda e: e.copy(out=lat[:, blk * 4 + cc, :], in_=ps[:]), reads=[pb],
                              writes=[latb[blk * 4 + cc]])
                r_, rb_ = norm_rstd([lat[:, k, :] for k in range(4)], latb[0:4], 512, 1)
                for k in range(4):
                    apply_norm(qn[:, k, :], qnb[k], lat[:, k, :], latb[k], V_QN + j * 4 + k, r_, rb_)
                r_, rb_ = norm_rstd([lat[:, 4 + k, :] for k in range(4)], latb[4:8], 512, 1)
                for k in range(4):
                    apply_norm(kvn[:, k, :], kvnb[k], lat[:, 4 + k, :], latb[4 + k], V_KVN + j * 4 + k, r_, rb_)
                slot, wv = kb.wblock("mla_w_in", j * D, D, 1024, 128)
                psa, pba = kb.psum()
                kb.mm(psa[0:64, :], [(wv[:, k, 0:64], xin(k)) for k in range(KC)], reads=[kb.wbuf[slot]] + xnb, writes=[pba])
                psb_, pbb = kb.psum()
                kb.mm(psb_[0:64, :], [(wv[:, k, 64:128], xin(k)) for k in range(KC)], reads=[kb.wbuf[slot]] + xnb, writes=[pbb])
                stage_out(lambda ap, b: rope(ap, b, psa, pba, psb_, pbb), 64, send1[KPE0:KPE0 + 64, t0:t0 + T])
                for blk in range(4):
                    slot, wv = kb.wblock("mla_w_in", j * D, D, 1152 + blk * 512, 512)
                    for cc in range(4):
                        m = blk * 4 + cc
                        ps, pb = kb.psum()
                        kb.mm(ps[:], [(wv[:, k, cc * 128:(cc + 1) * 128], xin(k)) for k in range(KC)],
                              reads=[kb.wbuf[slot]] + xnb, writes=[pb])
                        kb.op("act", lambda e: e.activation(out=sgt[:, m, :], in_=ps[:], func=AF.Silu),
                              reads=[pb], writes=[sgtb[m]])
                kb.dma("sp", gbuf.rearrange("(k p) t -> p k t", p=128)[:, :, t0:t0 + T], sgt[:], sgt_ds, reads=sgtb)
                slot, wv = kb.wblock("mla_w_uq", j * 512, 512, 0, 2048)
                for h in range(16):
                    ps, pb = kb.psum()
                    kb.mm(ps[:], [(wv[:, k, h * 128:(h + 1) * 128], qn[:, k, :]) for k in range(4)],
                          reads=[kb.wbuf[slot]] + qnb, writes=[pb])
                    stage_out(lambda ap, b: kb.op("act", lambda e: e.copy(out=ap, in_=ps[:]), reads=[pb], writes=[b]),
                              128, send1[h * HB:h * HB + 128, t0:t0 + T])
                slot, wv = kb.wblock("mla_w_uq", j * 512, 512, 2048, 2048)
                for h in range(16):
                    psa, pba = kb.psum()
                    kb.mm(psa[0:64, :], [(wv[:, k, h * 64:(h + 1) * 64], qn[:, k, :]) for k in range(4)],
                          reads=[kb.wbuf[slot]] + qnb, writes=[pba])
                    psb_, pbb = kb.psum()
                    kb.mm(psb_[0:64, :], [(wv[:, k, 1024 + h * 64:1024 + (h + 1) * 64], qn[:, k, :]) for k in range(4)],
                          reads=[kb.wbuf[slot]] + qnb, writes=[pbb])
                    stage_out(lambda ap, b: rope(ap, b, psa, pba, psb_, pbb), 64,
                              send1[h * HB + 128:h * HB + 192, t0:t0 + T])
                slot, wv = kb.wblock("mla_w_ukv", j * 512, 512, 0, 2048)
                for h in range(16):
                    ps, pb = kb.psum()
                    kb.mm(ps[:], [(wv[:, k, h * 128:(h + 1) * 128], kvn[:, k, :]) for k in range(4)],
                          reads=[kb.wbuf[slot]] + kvnb, writes=[pb])
                    stage_out(lambda ap, b: kb.op("act", lambda e: e.copy(out=ap, in_=ps[:]), reads=[pb], writes=[b]),
                              128, send1[h * HB + 192:h * HB + 320, t0:t0 + T])
                slot, wv = kb.wblock("mla_w_ukv", j * 512, 512, 2048, 2048)
                for tb in range(4):
                    for hg in range(4):
                        ps, pb = kb.psum()
                        kb.mm(ps[:], [(kvn[:, k, tb * 128:(tb + 1) * 128], wv[:, k, hg * 512:(hg + 1) * 512]) for k in range(4)],
                              reads=[kb.wbuf[slot]] + kvnb, writes=[pb])
                        kb.op("act", lambda e: e.copy(out=vst[:, tb, hg * 512:(hg + 1) * 512], in_=ps[:]),
                              reads=[pb], writes=[vstb[tb]])
                for tb in range(4):
                    kb.dma("sp", send1[V0:V0 + NTOK, :].rearrange("a b -> (a b)")
                           .rearrange("(q k p x) -> k p q x", q=NCORE, p=128, x=256)[ti * 4 + tb],
                           vst[:, tb, :].rearrange("p (q x) -> p q x", q=NCORE), vst_ds, reads=[vstb[tb]])
        kb.barrier(rotate=False)
        kb.allgather(send1, gath1)
        kb.barrier(rotate=False)

        with ExitStack() as st:
            def S_(n, sh, d):
                uid[0] += 1
                return st.enter_context(nc.sbuf_tensor(f"sb{uid[0]}_{n}", sh, d))
            Kn = [S_(f"Kn{i}", [128, SEQ], BF16) for i in range(2)]
            Kp = [S_(f"Kp{i}", [64, SEQ], BF16) for i in range(2)]
            Vt = [S_(f"Vt{i}", [128, 32, 128], BF16) for i in range(2)]
            kvb = bufs(2); kv_ds = [kb.dsem(f"kv{i}") for i in range(2)]
            Qn = [S_(f"Qn{i}", [128, T], BF16) for i in range(2)]
            Qp = [S_(f"Qp{i}", [64, T], BF16) for i in range(2)]
            qb = bufs(2); q_ds = [kb.dsem(f"q{i}") for i in range(2)]
            PT = [S_(f"PT{i}", [128, T], BF16) for i in range(4)]; ptb = bufs(4)
            rl = S_("rl", [128, T], F32); rlb = Buf()
            ost = [S_(f"ost{i}", [128, T], BF16) for i in range(2)]; ostb = bufs(2)
            o_ds = [kb.dsem(f"o{i}") for i in range(2)]
            pid = nc.sync.partition_id()
            pairs = [(b, hh) for b in range(4) for hh in range(2)]

            loc_ds = kb.dsem("loc")
            CH = 32768
            kb.dma("sp", myq.rearrange("r x t -> r (x t)").rearrange("r (a b) -> r a b", b=CH),
                   gath1.rearrange("(r y) t -> r y t", r=NCORE)[:, 0:16 * HB, :]
                   .rearrange("r (q x) t -> q r (x t)", q=NCORE)[bass.ds(pid, 1)]
                   .rearrange("o r (a b) -> (o r) a b", b=CH), loc_ds)
            kb.dma("sp", myv.rearrange("r k x -> r (k x)").rearrange("r (a b) -> r a b", b=CH),
                   gath1.rearrange("(r y) t -> r y t", r=NCORE)[:, V0:V0 + NTOK, :]
                   .rearrange("r a b -> r (a b)").rearrange("r (q n) -> q r n", q=NCORE)[bass.ds(pid, 1)]
                   .rearrange("o r (a b) -> (o r) a b", b=CH), loc_ds)
            kb.barrier(rotate=False)

            def load_kv(pi):
                b, hh = pairs[pi]
                s = pi % 2
                for sr in range(2):
                    r = 2 * b + sr
                    kb.dma("sp", Kn[s][:, sr * NTOK:(sr + 1) * NTOK], myq[r, hh * HB + 192:hh * HB + 320, :],
                           kv_ds[s], writes=[kvb[s]])
                    kb.dma("sp", Kp[s][:, sr * NTOK:(sr + 1) * NTOK],
                           gath1[r * R1 + KPE0:r * R1 + KPE0 + 64, :], kv_ds[s], writes=[kvb[s]])
                    kb.dma("sp", Vt[s][:, sr * 16:(sr + 1) * 16, :],
                           myv[r, :, hh * 128:(hh + 1) * 128].rearrange("(k p) x -> p k x", p=128),
                           kv_ds[s], writes=[kvb[s]])

            qcount = [0]

            def load_q(pi, qt):
                b, hh = pairs[pi]
                s = qcount[0] % 2
                qcount[0] += 1
                r = 2 * b + qt // 4
                c0 = (qt % 4) * T
                kb.dma("sp", Qn[s][:], myq[r, hh * HB:hh * HB + 128, c0:c0 + T], q_ds[s], writes=[qb[s]])
                kb.dma("sp", Qp[s][:], myq[r, hh * HB + 128:hh * HB + 192, c0:c0 + T], q_ds[s], writes=[qb[s]])
                return s

            work = []
            for pi in range(len(pairs)):
                for qt in range(8):
                    nkb = 4 * (qt + 1)
                    for kbi in range(nkb):
                        work.append((pi, qt, kbi, nkb))
            load_kv(0)
            qslot = {}
            qslot[(0, 0)] = load_q(0, 0)
            sbank = [4, 5, 6, 7]
            pend = []
            oacc = {}
            nq = [0]

            def emit_pv(item):
                pi, qt, kbi, nkb, pti, c0 = item
                s = pi % 2
                key = (pi, qt)
                if key not in oacc:
                    oacc[key] = nq[0] % 2
                    nq[0] += 1
                a = oacc[key]
                po, pob = kb.ps[a], kb.psb[a]
                pl, plb_ = kb.ps[2 + a], kb.psb[2 + a]
                kb.mm(po[:, c0:T], [(Vt[s][:, kbi, :], PT[pti][:, c0:T])], reads=[kvb[s], ptb[pti]], writes=[pob],
                      start=(kbi == 0), stop=(kbi == nkb - 1))
                kb.mm(pl[:, c0:T], [(onesb[:], PT[pti][:, c0:T])], reads=[ones_b, ptb[pti]], writes=[plb_],
                      start=(kbi == 0), stop=(kbi == nkb - 1))
                if kbi == nkb - 1:
                    b, hh = pairs[pi]
                    kb.op("dve", lambda e: e.reciprocal(out=rl[:], in_=pl[:]), reads=[plb_], writes=[rlb])
                    oi = a
                    kb.op("dve", lambda e: e.tensor_tensor(out=ost[oi][:], in0=po[:], in1=rl[:], op=ALU.mult),
                          reads=[pob, rlb], writes=[ostb[oi]])
                    row0 = (b * 2 + qt // 4) * 256 + hh * 128
                    kb.dma("sp", send2[row0:row0 + 128, (qt % 4) * T:(qt % 4 + 1) * T], ost[oi][:],
                           o_ds[oi], reads=[ostb[oi]])

            for wi_, (pi, qt, kbi, nkb) in enumerate(work):
                s = pi % 2
                if kbi == 0:
                    nxt = (pi, qt + 1) if qt < 7 else ((pi + 1, 0) if pi + 1 < len(pairs) else None)
                    if nxt is not None:
                        qslot[nxt] = load_q(*nxt)
                if qt == 0 and kbi == 3 and pi + 1 < len(pairs):
                    load_kv(pi + 1)
                qs = qslot[(pi, qt)]
                d = kbi - 4 * qt
                c0 = 128 * d if d > 0 else 0
                sbk = sbank[wi_ % 4]
                psS, psSb = kb.ps[sbk], kb.psb[sbk]
                kb.mm(psS[:, c0:T], [(Kn[s][:, kbi * 128:(kbi + 1) * 128], Qn[qs][:, c0:T]),
                                     (Kp[s][:, kbi * 128:(kbi + 1) * 128], Qp[qs][:, c0:T])],
                      reads=[kvb[s], qb[qs]], writes=[psSb])
                pti = wi_ % 4
                kb.op("act", lambda e: e.activation(out=PT[pti][:, c0:T], in_=psS[:, c0:T], func=AF.Exp, scale=SM_SCALE),
                      reads=[psSb], writes=[ptb[pti]])
                if d >= 0:
                    kb.op("dve", lambda e: e.tensor_tensor(out=PT[pti][:, c0:c0 + 128], in0=PT[pti][:, c0:c0 + 128],
                                                           in1=tri[:], op=ALU.mult), reads=[ptb[pti], trib], writes=[ptb[pti]])
                pend.append((pi, qt, kbi, nkb, pti, c0))
                if len(pend) > 2:
                    emit_pv(pend.pop(0))
            while pend:
                emit_pv(pend.pop(0))
        kb.barrier(rotate=False)
        kb.allgather(send2, gath2)
        kb.barrier(rotate=False)

        with ExitStack() as st:
            def S_(n, sh, d):
                uid[0] += 1
                return st.enter_context(nc.sbuf_tensor(f"sb{uid[0]}_{n}", sh, d))
            hT = S_("hT", [128, KC, T], F32); hb = bufs(KC)
            ot = S_("ot", [128, KC, T], F32); otb = bufs(KC)
            otv = ot[:].bitcast(BF16)
            xn = lambda kc: otv[:, kc // 2, (kc % 2) * T:(kc % 2 + 1) * T]
            og = S_("og", [128, KC, T], BF16); ogb = bufs(KC); og_ds = kb.dsem("og")
            sgc = S_("sgc", [128, KC, T], BF16); sgcb = bufs(KC); sgc_ds = kb.dsem("sgc")
            pT = S_("pT", [128, 2, T], BF16); pT_b = Buf()
            gt = [S_(f"gt{i}", [128, T], F32) for i in range(2)]; gt_b = bufs(2)
            wproj = S_("wproj", [128, 2, D], BF16); wproj_b = Buf()
            tl_ds = kb.dsem("tail")
            load_layer_consts(l, wproj, wproj_b)
            pid = nc.sync.partition_id()
            kb.dma("sp", myo.rearrange("r x t -> r (x t)").rearrange("r (a b) -> r a b", b=32768),
                   gath2.rearrange("(r c x) t -> c r (x t)", r=NCORE, c=NCORE)[bass.ds(pid, 1)]
                   .rearrange("o r (a b) -> (o r) a b", b=32768), kb.dsem("loc"))
            kb.barrier(rotate=False)
            for ti in range(NTI):
                t0 = ti * T
                load_h(hT, hb, src, t0)
                load_p(l, pT, pT_b, t0)
                for r in range(NCORE):
                    kb.dma("sp", og[:, 2 * r:2 * r + 2, :],
                           myo[r].rearrange("(h p) t -> p h t", p=128)[:, :, t0:t0 + T], og_ds,
                           writes=ogb[2 * r:2 * r + 2])
                for b_ in ogb:
                    b_.w = ogb[-1].w
                kb.dma("sp", sgc[:], gbuf.rearrange("(k p) t -> p k t", p=128)[:, :, t0:t0 + T], sgc_ds, writes=sgcb)
                for h in range(KC):
                    kb.op("dve", lambda e: e.tensor_tensor(out=og[:, h, :], in0=og[:, h, :], in1=sgc[:, h, :], op=ALU.mult),
                          reads=[ogb[h], sgcb[h]], writes=[ogb[h]])
                for blk in range(4):
                    slot, wv = kb.wblock("mla_w_out", j * D, D, blk * 512, 512)
                    for cc in range(4):
                        m = blk * 4 + cc
                        ps, pb = kb.psum()
                        kb.mm(ps[:], [(wv[:, k, cc * 128:(cc + 1) * 128], og[:, k, :]) for k in range(KC)],
                              reads=[kb.wbuf[slot]] + ogb, writes=[pb])
                        kb.op("act", lambda e: e.copy(out=ot[:, m, :], in_=ps[:]), reads=[pb], writes=[otb[m]])
                post_and_ple(l, hT, hb, ot, otb, xn, pT, pT_b, wproj, wproj_b, gt, gt_b)
                store_h(hT, hb, dst, t0)
                if write_tail and ti == NTI - 1:
                    kb.dma("sp", tail.rearrange("(k p) t -> p k t", p=128), hT[:, :, T - 16:T], tl_ds, reads=hb)
        if write_tail:
            kb.barrier(rotate=False)
            kb.allgather(tail, gath_t)
        kb.barrier()

    for l in range(l0, l0 + nlayers):
        src = x_fm if l == l0 else hbuf
        dst = out_fm if l == l0 + nlayers - 1 else hbuf
        if l % 2 == 0:
            pool_layer(l, l // 2, src, dst, halo_from_gath=(l > l0))
        else:
            mla_layer(l, l // 2, src, dst, write_tail=(l + 1 < l0 + nlayers))
    kb.barrier(rotate=False)
    return kb


def _prep_inputs(inp):
    f32 = np.float32
    x = np.asarray(inp["x"], f32)
    p = np.asarray(inp["p"], f32)
    positions = np.asarray(inp["positions"])
    W = {}
    W["pool_w_in"] = np.asarray(inp["pool_w_in"], f32).reshape(2 * 2048, 8192)
    W["pool_w_group"] = np.asarray(inp["pool_w_group"], f32).reshape(2 * 4 * 1024, 1024)
    W["pool_w_out"] = np.asarray(inp["pool_w_out"], f32).reshape(2 * 4096, 2048)
    wi = np.asarray(inp["mla_w_in"], f32)
    kpe = wi[:, :, 1024:1088]
    kpe_sw = np.concatenate([kpe[:, :, 32:64], kpe[:, :, 0:32]], axis=-1)
    W["mla_w_in"] = np.concatenate([wi[:, :, 0:1088], kpe_sw, wi[:, :, 1088:]], axis=-1).reshape(2 * 2048, 3200)
    wq = np.asarray(inp["mla_w_uq"], f32).reshape(2, 512, 16, 192)
    qpe = wq[:, :, :, 128:192]
    qpe_sw = np.concatenate([qpe[..., 32:64], qpe[..., 0:32]], axis=-1)
    W["mla_w_uq"] = np.concatenate([wq[:, :, :, 0:128].reshape(2, 512, 2048), qpe.reshape(2, 512, 1024),
                                    qpe_sw.reshape(2, 512, 1024)], axis=-1).reshape(2 * 512, 4096)
    wkv = np.asarray(inp["mla_w_ukv"], f32).reshape(2, 512, 16, 256)
    W["mla_w_ukv"] = np.concatenate([wkv[:, :, :, 0:128].reshape(2, 512, 2048),
                                     wkv[:, :, :, 128:256].reshape(2, 512, 2048)], axis=-1).reshape(2 * 512, 4096)
    W["mla_w_out"] = np.asarray(inp["mla_w_out"], f32).reshape(2 * 2048, 2048)
    W["ple_w_gate"] = np.asarray(inp["ple_w_gate"], f32).reshape(4 * 2048, 2048)
    W["ple_w_proj"] = np.asarray(inp["ple_w_proj"], f32).reshape(4 * 256, 2048)
    for n, (r, c) in WSHAPES.items():
        assert W[n].shape == (r, c), (n, W[n].shape)

    def chunked(v):
        L, F = v.shape
        return np.asarray(v, f32).reshape(L, F // 128, 128).transpose(2, 0, 1).reshape(128, L * (F // 128))

    vbase = np.zeros((128, NV), f32)
    vbase[:, V_PRE:V_PRE + 64] = chunked(inp["pre_norm"])
    vbase[:, V_POST:V_POST + 64] = chunked(inp["post_norm"])
    vbase[:, V_PLE:V_PLE + 64] = chunked(inp["ple_norm"])
    vbase[:, V_SCALE:V_SCALE + 64] = chunked(inp["pool_scale"])
    vbase[:, V_QN:V_QN + 8] = chunked(inp["mla_q_norm"])
    vbase[:, V_KVN:V_KVN + 8] = chunked(inp["mla_kv_norm"])
    inv_freq = (10000.0 ** (-np.arange(0, 64, 2, dtype=np.float32) / 64)).astype(f32)
    vbase[0:64, V_IFREQ] = np.concatenate([inv_freq, inv_freq])
    vbase[0:32, V_SSIGN] = -1.0
    vbase[32:64, V_SSIGN] = 1.0

    in_maps = []
    for c in range(NCORE):
        b, half = c // 2, c % 2
        t0 = half * NTOK
        m = {}
        m["x_fm"] = np.ascontiguousarray(x[b, t0:t0 + NTOK, :].T)
        xh = np.zeros((D, 16), f32)
        if half == 1:
            xh[:, :] = x[b, t0 - 16:t0, :].T
        m["xhalo"] = xh
        m["p_fm"] = np.ascontiguousarray(p[:, b, t0:t0 + NTOK, :].transpose(0, 2, 1)).reshape(4 * 256, NTOK)
        m["pos"] = np.ascontiguousarray(np.broadcast_to(positions[b, t0:t0 + NTOK].astype(np.int32)[None, :], (64, NTOK)))
        v = vbase.copy()
        v[:, V_HALF] = float(half)
        for g in range(4):
            w = 2 ** (g + 1)
            for i in range(16):
                v[:, V_IC + g * 16 + i] = 1.0 / (min(i + 1, w) if half == 0 else w)
        m["vecs"] = v
        for n, (r, cc) in WSHAPES.items():
            rs = r // NCORE
            m["w_" + n] = np.ascontiguousarray(W[n][c * rs:(c + 1) * rs])
        in_maps.append(m)
    return in_maps


_CACHE = {}


def _get_program(nlayers, l0=0):
    if (nlayers, l0) not in _CACHE:
        k1 = build(nlayers, plan=None, l0=l0)
        k2 = build(nlayers, plan=k1.wrec, l0=l0)
        _CACHE[(nlayers, l0)] = k2
    return _CACHE[(nlayers, l0)]


def run(inputs, nlayers=4, l0=0):
    in_maps = _prep_inputs(inputs)
    kb = _get_program(nlayers, l0)
    res = run_bass_kernel_spmd(kb.nc, in_maps, core_ids=list(range(NCORE)))
    out = np.empty((4, SEQ, D), np.float32)
    for c in range(NCORE):
        b, half = c // 2, c % 2
        out[b, half * NTOK:(half + 1) * NTOK, :] = res.results[c]["out_fm"].T
    return out


def kernel(**inputs):
    return run(inputs, 4, 0)
```
